# Optimizing a Trainium2 kernel written in Bass

```python
import jax, jax.numpy as jnp
from jax import lax
import numpy as np

D_MODEL = 2048
BATCH = 4
SEQ = 2048
DEPTH = 4
DEC_BATCH = 128
DEC_SEQ = 4
PAST_LEN = 16384
PAGE_SIZE = 128

D_A = D_MODEL // 2
D_B = D_MODEL
CONV_A_W = 3
CONV_B_W = 4
N_BLOCKS = 16
BLOCK = D_B // N_BLOCKS
C_LRU = 8.0
FFN_HIDDEN = -(-8 * D_MODEL // (3 * 256)) * 256
EPS = 1e-6
IN_COLS = 3 * D_A + 2 * D_B + 2 * D_MODEL
SPLITS = [D_A, 2 * D_A, 3 * D_A, 3 * D_A + D_B, 3 * D_A + 2 * D_B, 3 * D_A + 2 * D_B + D_MODEL]

kernel_name = "hybrid_conv_rglru_decoder_step"


def rms_norm(x, g):
    xf = x.astype(jnp.float32)
    var = jnp.mean(xf * xf, axis=-1, keepdims=True)
    return (xf * lax.rsqrt(var + EPS) * g.astype(jnp.float32)).astype(x.dtype)


def causal_dwconv(u, buf, w):
    width = w.shape[0]
    t = u.shape[1]
    full = jnp.concatenate([buf.astype(u.dtype), u], axis=1)
    y = full[:, 0:t] * w[0]
    for k in range(1, width):
        y = y + full[:, k:k + t] * w[k]
    return y, full[:, full.shape[1] - (width - 1):]


def block_diag(x, w, b):
    xb = x.reshape(x.shape[0], x.shape[1], N_BLOCKS, BLOCK)
    y = jnp.einsum('bthi,hij->bthj', xb, w)
    return y.reshape(x.shape) + b


def rg_lru(x, h0, w_r, b_r, w_i, b_i, lam):
    r = jax.nn.sigmoid(block_diag(x, w_r, b_r).astype(jnp.float32))
    i = jax.nn.sigmoid(block_diag(x, w_i, b_i).astype(jnp.float32))
    log_a = C_LRU * r * jax.nn.log_sigmoid(lam.astype(jnp.float32))
    a = jnp.exp(log_a)
    mult = jnp.sqrt(-jnp.expm1(2.0 * log_a))
    b = mult * i * x.astype(jnp.float32)
    b = b.at[:, 0].add(a[:, 0] * h0.astype(jnp.float32))

    def combine(left, right):
        a1, b1 = left
        a2, b2 = right
        return a1 * a2, a2 * b1 + b2

    _, h = lax.associative_scan(combine, (a, b), axis=1)
    return h.astype(x.dtype), h[:, -1].astype(h0.dtype)


def hybrid_layer(x, buf_a, buf_b, h0, w_in, conv_a_w, w_out_a, conv_b_w, conv_b_bias,
                 w_r, b_r, w_i, b_i, lam, w_out_b, w_o, g_pre_mix, g_post_mix,
                 g_pre_ffn, g_post_ffn, w_gate_up, w_down):
    u = rms_norm(x, g_pre_mix)
    proj = jnp.einsum('btd,dp->btp', u, w_in)
    a_bg, a_cg, a_x, b_x, b_gate, gate_a, gate_b = jnp.split(proj, SPLITS, axis=-1)
    conv_out, new_a = causal_dwconv(a_cg * a_x, buf_a, conv_a_w)
    y_a = jnp.einsum('btc,cd->btd', a_bg * conv_out, w_out_a)
    xb, new_b = causal_dwconv(b_x, buf_b, conv_b_w)
    xb = xb + conv_b_bias
    h_seq, h_last = rg_lru(xb, h0, w_r, b_r, w_i, b_i, lam)
    y_b = jnp.einsum('btc,cd->btd', h_seq * jax.nn.gelu(b_gate), w_out_b)
    merged = jax.nn.sigmoid(gate_a) * y_a + jax.nn.sigmoid(gate_b) * y_b
    mix = jnp.einsum('btd,de->bte', merged, w_o)
    x = x + rms_norm(mix, g_post_mix)
    v = rms_norm(x, g_pre_ffn)
    gu = jnp.einsum('btd,df->btf', v, w_gate_up)
    g, up = jnp.split(gu, 2, axis=-1)
    f = jnp.einsum('btf,fd->btd', jax.nn.silu(g) * up, w_down)
    x = x + rms_norm(f, g_post_ffn)
    return x, new_a, new_b, h_last


def setup_inputs(seed: int = 0) -> dict:
    key = jax.random.key(seed)
    ks = jax.random.split(key, 24)
    nrm = lambda k, shape, s: jax.random.normal(k, shape, jnp.float32) * s
    u = jax.random.uniform(ks[10], (DEPTH, D_B), jnp.float32, 0.9, 0.999)
    p = u ** (1.0 / C_LRU)
    lam = jnp.log(p) - jnp.log1p(-p)
    return {
        "x_prompt": nrm(ks[0], (BATCH, SEQ, D_MODEL), 1.0),
        "x_sample": nrm(ks[1], (DEC_BATCH, DEC_SEQ, D_MODEL), 1.0),
        "state_conv_a": nrm(ks[2], (DEPTH, DEC_BATCH, CONV_A_W - 1, D_A), 1.0),
        "state_conv_b": nrm(ks[3], (DEPTH, DEC_BATCH, CONV_B_W - 1, D_B), 1.0),
        "state_lru_h": nrm(ks[4], (DEPTH, DEC_BATCH, D_B), 0.5),
        "w_in": nrm(ks[5], (DEPTH, D_MODEL, IN_COLS), D_MODEL ** -0.5),
        "conv_a_w": nrm(ks[6], (DEPTH, CONV_A_W, D_A), CONV_A_W ** -0.5),
        "w_out_a": nrm(ks[7], (DEPTH, D_A, D_MODEL), D_A ** -0.5),
        "conv_b_w": nrm(ks[8], (DEPTH, CONV_B_W, D_B), CONV_B_W ** -0.5),
        "conv_b_bias": nrm(ks[9], (DEPTH, D_B), 0.02),
        "w_r": nrm(ks[11], (DEPTH, N_BLOCKS, BLOCK, BLOCK), BLOCK ** -0.5),
        "b_r": nrm(ks[12], (DEPTH, D_B), 0.02),
        "w_i": nrm(ks[13], (DEPTH, N_BLOCKS, BLOCK, BLOCK), BLOCK ** -0.5),
        "b_i": nrm(ks[14], (DEPTH, D_B), 0.02),
        "lru_lambda": lam,
        "w_out_b": nrm(ks[15], (DEPTH, D_B, D_MODEL), D_B ** -0.5),
        "w_o": nrm(ks[16], (DEPTH, D_MODEL, D_MODEL), D_MODEL ** -0.5),
        "norm_pre_mix": 1.0 + nrm(ks[17], (DEPTH, D_MODEL), 0.02),
        "norm_post_mix": 1.0 + nrm(ks[18], (DEPTH, D_MODEL), 0.02),
        "norm_pre_ffn": 1.0 + nrm(ks[19], (DEPTH, D_MODEL), 0.02),
        "norm_post_ffn": 1.0 + nrm(ks[20], (DEPTH, D_MODEL), 0.02),
        "w_gate_up": nrm(ks[21], (DEPTH, D_MODEL, 2 * FFN_HIDDEN), D_MODEL ** -0.5),
        "w_down": nrm(ks[22], (DEPTH, FFN_HIDDEN, D_MODEL), FFN_HIDDEN ** -0.5),
    }


def reference(x_prompt, x_sample, state_conv_a, state_conv_b, state_lru_h, w_in, conv_a_w,
              w_out_a, conv_b_w, conv_b_bias, w_r, b_r, w_i, b_i, lru_lambda, w_out_b, w_o,
              norm_pre_mix, norm_post_mix, norm_pre_ffn, norm_post_ffn, w_gate_up, w_down):
    bp = x_prompt.shape[0]
    dt = x_prompt.dtype
    zero_a = jnp.zeros((bp, CONV_A_W - 1, D_A), dt)
    zero_b = jnp.zeros((bp, CONV_B_W - 1, D_B), dt)
    zero_h = jnp.zeros((bp, D_B), state_lru_h.dtype)
    xp, xs = x_prompt, x_sample
    pa, pb, ph, sa, sb, sh = [], [], [], [], [], []
    for l in range(DEPTH):
        w = (w_in[l], conv_a_w[l], w_out_a[l], conv_b_w[l], conv_b_bias[l], w_r[l], b_r[l],
             w_i[l], b_i[l], lru_lambda[l], w_out_b[l], w_o[l], norm_pre_mix[l],
             norm_post_mix[l], norm_pre_ffn[l], norm_post_ffn[l], w_gate_up[l], w_down[l])
        xp, na, nb, nh = hybrid_layer(xp, zero_a, zero_b, zero_h, *w)
        pa.append(na); pb.append(nb); ph.append(nh)
        xs, na, nb, nh = hybrid_layer(xs, state_conv_a[l], state_conv_b[l], state_lru_h[l], *w)
        sa.append(na); sb.append(nb); sh.append(nh)
    return (xp, xs, jnp.stack(pa), jnp.stack(pb), jnp.stack(ph),
            jnp.stack(sa), jnp.stack(sb), jnp.stack(sh))
```

```python
import numpy as np
import concourse.bass as bass
import concourse.mybir as mybir
from concourse.bass_utils import run_bass_kernel_spmd

F32 = mybir.dt.float32
BF16 = mybir.dt.bfloat16
AF = mybir.ActivationFunctionType
ALU = mybir.AluOpType

DEPTH = 4
DM = 2048
NP_ = 1024
NS = 64
T = NP_ + NS
TILES = [(0, 512), (512, 512), (1024, 64)]
KC = 16
DA = 1024
FF = 5632
NFC = FF // 128
IN_COLS = 11264
EPS = 1e-6
NPAR = 15
SOW = 102
NWB = 3
FBLK = [(0, 16), (16, 16), (32, 12)]
O_BG, O_CG, O_AX, O_BX, O_BGATE, O_GA, O_GB = 0, 1024, 2048, 3072, 5120, 7168, 9216


class Buf:
    __slots__ = ("w", "r")

    def __init__(self):
        self.w = None
        self.r = []


class Plan:
    def __init__(self):
        self.q = {e: [] for e in ("pe", "act", "dve", "pool", "sp")}
        self.cnt = {}
        self.waited = {e: {} for e in self.q}

    def emit(self, eng, fn, reads=(), writes=(), sig=None, inc=1, extra=()):
        deps = list(extra)
        for b in reads:
            deps.append(b.w)
        for b in writes:
            deps.append(b.w)
            deps.extend(b.r)
        waits = []
        wd = self.waited[eng]
        for d in deps:
            if d is None:
                continue
            k, v = d
            if wd.get(k, 0) < v:
                wd[k] = v
                waits.append((k, v))
        if sig is None:
            sig = {"pe": "S_pe", "act": "S_act", "dve": "S_dve"}[eng]
        self.cnt[sig] = self.cnt.get(sig, 0) + inc
        tok = (sig, self.cnt[sig])
        self.q[eng].append((waits, fn, sig, inc))
        for b in reads:
            b.r.append(tok)
        for b in writes:
            b.w = tok
            b.r = []
        return tok


DEBUG = {}


def build_program():
    nc = bass.Bass("TRN2", target_bir_lowering=False)
    DEPTH = DEBUG.get("depth", 4)
    STOP = DEBUG.get("stop", 99)

    def din(name, shape):
        return nc.dram_tensor(name, list(shape), F32, kind="ExternalInput").ap()

    xT = din("xT", [DM, T])
    stA = din("stA", [DEPTH, DA, 32])
    stB = din("stB", [DEPTH, DM, 48])
    stH = din("stH", [DEPTH, DM, 16])
    par = din("par", [128, DEPTH * NPAR * KC])
    maskd = din("mask", [128, 1])
    w_in = din("w_in", [DEPTH, DM, IN_COLS])
    LITE = DEBUG.get("lite", False)
    ASTOP_ = DEBUG.get("astop", 99)
    JSTOP = DEBUG.get("jstop", 0)
    w_out_a = din("w_out_a", [DEPTH, DA, DM] if not LITE else [DEPTH, 128, 128])
    w_r = din("w_r", [DEPTH, 16, 128, 128])
    w_i = din("w_i", [DEPTH, 16, 128, 128])
    w_out_b = din("w_out_b", [DEPTH, DM, DM] if not LITE else [DEPTH, 128, 128])
    w_o = din("w_o", [DEPTH, DM, DM] if not LITE else [DEPTH, 128, 128])
    w_gu = din("w_gate_up", [DEPTH, DM, 2 * FF] if not LITE else [DEPTH, 128, 128])
    w_down = din("w_down", [DEPTH, FF, DM] if not LITE else [DEPTH, 128, 128])
    yT = nc.dram_tensor("yT", [DM, T], F32, kind="ExternalOutput").ap()
    sto = nc.dram_tensor("sto", [DEPTH, DM, SOW], F32, kind="ExternalOutput").ap()
    xs = nc.dram_tensor("xspill", [DM, T], F32).ap()
    cc_in = nc.dram_tensor("cc_in", [128, KC * 6], F32)
    cc_out = nc.dram_tensor("cc_out", [256, KC * 6], F32, addr_space="Local")
    pairs = [[0, 1], [2, 3], [4, 5], [6, 7]]

    P = Plan()
    SEMS = ["S_pe", "S_act", "S_dve", "S_pool", "D_misc", "D_spill", "D_xs0", "D_xs1", "D_st",
            "D_out", "D_xch", "D_gw", "CC"] + ["D_w%d" % i for i in range(NWB)] + \
           ["D_x%d" % i for i in range(4)]

    from contextlib import ExitStack
    with ExitStack() as es:
        def sb(name, shape, dt=F32):
            return es.enter_context(nc.sbuf_tensor(name, list(shape), dt))

        Bt = sb("Bt", [128, KC * T])
        CDt = sb("CDt", [128, KC * T])
        Wt = sb("Wt", [128, NWB, KC, 128], BF16)
        GWt = sb("GWt", [128, 2, 16, 128], BF16)
        TMPt = sb("TMPt", [128, 6, 1140])
        RSt = sb("RSt", [128, T])
        PARt = sb("PARt", [128, DEPTH * NPAR * KC])
        SOt = sb("SOt", [128, KC, SOW])
        ONESt = sb("ONESt", [128, 128], BF16)
        XIt = sb("XIt", [128, KC, 6])
        XRt = sb("XRt", [128, KC, 6])
        BNDt = sb("BNDt", [128, KC, 6])
        CVt = sb("CVt", [128, DEPTH, 3, KC])
        MASKt = sb("MASKt", [128, 1])
        SAt = sb("SAt", [128, 8, 32])
        SBt = sb("SBt", [128, KC, 48])
        SHt = sb("SHt", [128, KC, 16])
        ABGt = sb("ABGt", [128, 8, 2])
        CA32t = sb("CA32t", [128, 8, 2])
        FIXt = sb("FIXt", [128, 8, 4])
        PSt = es.enter_context(nc.psum_tensor("PSt", [128, 8, 512], F32))
        sem = {k: es.enter_context(nc.semaphore(k)) for k in SEMS}

        Bf = Bt[:].rearrange("p (c t) -> p c t", t=T)
        Bh = Bt[:].bitcast(BF16).rearrange("p (c t) -> p c t", t=T)
        CDf = CDt[:].rearrange("p (c t) -> p c t", t=T)
        CDh = CDt[:].bitcast(BF16).rearrange("p (c t) -> p c t", t=T)
        PARv = PARt[:].rearrange("p (l n c) -> p l n c", l=DEPTH, n=NPAR)
        PSf = PSt[:].rearrange("p b n -> p (b n)")

        def Cc(j):
            return CDh[:, j, :]

        def Dc(j):
            return CDh[:, 16 + j, :]

        def merged(j):
            return Bh[:, j, :]

        def cach(j):
            return Bh[:, 16 + j, :]

        def sqB(j):
            return Bh[:, 16 + j, :]

        def ps(slot, lo=0, n=T):
            return PSf[:, slot * 2048 + lo: slot * 2048 + lo + n]

        def tmp(i, lo=0, n=T):
            return TMPt[:, i, lo:lo + n]

        def tmph(i, lo=0, n=T):
            return TMPt[:, i, :].bitcast(BF16)[:, lo:lo + n]

        def pcol(l, idx, j):
            return PARv[:, l, idx, j:j + 1]

        bB = [Buf() for _ in range(KC)]
        bC = [Buf() for _ in range(KC)]
        bD = [Buf() for _ in range(KC)]
        bW = [Buf() for _ in range(NWB)]
        bGW = Buf()
        bT = [Buf() for _ in range(6)]
        bRS = Buf()
        bPS = [Buf(), Buf()]
        bSO = Buf()
        bXI, bXR, bBND, bCV, bPAR, bMASK = Buf(), Buf(), Buf(), Buf(), Buf(), Buf()
        bSA, bSB, bSH = Buf(), Buf(), Buf()
        bABG, bCA32, bFIX = Buf(), Buf(), Buf()
        bXS = Buf()
        bCC = Buf()
        bONES = Buf()

        def bBh(k):
            return bB[k // 2]

        wstate = {"n": 0}

        def wload(src_ap, kc):
            n = wstate["n"]
            wstate["n"] += 1
            bi = n % NWB
            P.emit("pool", lambda e, bi=bi, src_ap=src_ap, kc=kc:
                   e.dma_start(out=Wt[:, bi, 0:kc, :], in_=src_ap),
                   writes=[bW[bi]], sig="D_w%d" % bi, inc=16)
            return bi

        def wsrc(wap, l, col, kc=KC, row0=0):
            v = wap[l, row0:row0 + kc * 128, col:col + 128]
            return v.rearrange("(k p) n -> p k n", p=128)

        pstate = {"n": 0}

        def mm_group(bi, kc, rhs_fn, lo=0, n=T, extra_reads=()):
            slot = pstate["n"] % 2
            pstate["n"] += 1
            tiles = []
            for (t0, tn) in TILES:
                a, b = max(t0, lo), min(t0 + tn, lo + n)
                if b > a:
                    tiles.append((a, b - a))
            def fn(e, bi=bi, kc=kc, rhs_fn=rhs_fn, slot=slot, tiles=tiles):
                ins = None
                for k in range(kc):
                    for (a, m) in tiles:
                        ins = e.matmul(ps(slot, a, m), Wt[:, bi, k, :], rhs_fn(k)[:, a:a + m],
                                       start=(k == 0), stop=(k == kc - 1))
                return ins
            P.emit("pe", fn, reads=[bW[bi]] + list(extra_reads), writes=[bPS[slot]])
            return slot

        def act(fn, reads=(), writes=(), extra=()):
            return P.emit("act", fn, reads=reads, writes=writes, extra=extra)

        def dve(fn, reads=(), writes=(), extra=()):
            return P.emit("dve", fn, reads=reads, writes=writes, extra=extra)

        def sp_dma(fn, sig, reads=(), writes=()):
            return P.emit("sp", fn, reads=reads, writes=writes, sig=sig, inc=16)

        sp_dma(lambda e: e.dma_start(out=PARt[:], in_=par[:, :]), "D_misc", writes=[bPAR])
        bPAR.w = sp_dma(lambda e: e.dma_start(out=MASKt[:], in_=maskd[:, :]), "D_misc", writes=[bMASK])
        xTv = xT.rearrange("(c p) t -> p c t", p=128)
        ltk = {}
        for j in range(KC):
            ltk[j % 4] = sp_dma(lambda e, j=j: e.dma_start(out=Bf[:, j, :], in_=xTv[:, j, :]),
                                "D_x%d" % (j % 4), writes=[bB[j]])
        for j in range(KC):
            bB[j].w = ltk[j % 4]
        dve(lambda e: e.memset(ONESt[:], 1.0), writes=[bONES])
        for l in range(DEPTH):
            act(lambda e, l=l: e.activation(out=CVt[:, l, 2, :], in_=PARv[:, l, 7, :], func=AF.Exp,
                                            scale=-1.0), reads=[bPAR], writes=[bCV])
            dve(lambda e, l=l: e.tensor_scalar(out=CVt[:, l, 2, :], in0=CVt[:, l, 2, :], scalar1=1.0,
                                               scalar2=None, op0=ALU.add), writes=[bCV])
            act(lambda e, l=l: e.activation(out=CVt[:, l, 2, :], in_=CVt[:, l, 2, :], func=AF.Ln),
                writes=[bCV])
            dve(lambda e, l=l: e.tensor_scalar(out=CVt[:, l, 0, :], in0=CVt[:, l, 2, :], scalar1=-8.0,
                                               scalar2=None, op0=ALU.mult), writes=[bCV])
            dve(lambda e, l=l: e.tensor_scalar(out=CVt[:, l, 1, :], in0=CVt[:, l, 2, :], scalar1=-16.0,
                                               scalar2=None, op0=ALU.mult), writes=[bCV])

        def norm_stats(src_fn, src_bufs, sq_fn, sq_bufs):
            for j in range(KC):
                act(lambda e, j=j: e.activation(out=sq_fn(j), in_=src_fn(j), func=AF.Square),
                    reads=[src_bufs[j]], writes=[sq_bufs[j]])
            slot = pstate["n"] % 2
            pstate["n"] += 1
            def fn(e, slot=slot):
                ins = None
                for k in range(KC):
                    for (a, m) in TILES:
                        ins = e.matmul(ps(slot, a, m), ONESt[:], sq_fn(k)[:, a:a + m],
                                       start=(k == 0), stop=(k == KC - 1))
                return ins
            P.emit("pe", fn, reads=list(sq_bufs) + [bONES], writes=[bPS[slot]])
            act(lambda e, slot=slot: e.activation(out=RSt[:], in_=ps(slot), func=AF.Sqrt,
                                                  scale=1.0 / DM, bias=EPS),
                reads=[bPS[slot]], writes=[bRS])
            dve(lambda e: e.reciprocal(out=RSt[:], in_=RSt[:]), writes=[bRS])

        def spill_x():
            xsv = xs.rearrange("(c p) t -> p c t", p=128)
            tk = None
            for j in range(KC):
                tk = sp_dma(lambda e, j=j: e.dma_start(out=xsv[:, j, :], in_=Bf[:, j, :]), "D_spill",
                            reads=[bB[j]], writes=[bXS])
            for j in range(KC):
                bB[j].r = [t for t in bB[j].r if t[0] != "D_spill"] + [tk]

        def residual_update(l, src_fn, src_bufs, gidx):
            xsv = xs.rearrange("(c p) t -> p c t", p=128)
            for j in range(KC):
                st = 4 + (j % 2)
                sp_dma(lambda e, j=j, st=st: e.dma_start(out=tmp(st), in_=xsv[:, j, :]),
                       "D_xs%d" % (j % 2), reads=[bXS], writes=[bT[st]])
                dve(lambda e, j=j: e.scalar_tensor_tensor(out=tmp(3), in0=src_fn(j),
                                                          scalar=pcol(l, gidx, j), in1=RSt[:],
                                                          op0=ALU.mult, op1=ALU.mult),
                    reads=[src_bufs[j], bRS, bPAR], writes=[bT[3]])
                dve(lambda e, j=j, st=st: e.tensor_tensor(out=Bf[:, j, :], in0=tmp(3), in1=tmp(st),
                                                          op=ALU.add),
                    reads=[bT[3], bT[st]], writes=[bB[j]])

        def lru_chunk(l, j, final):
            n = T if final else NP_
            bi = wload(wsrc(w_in, l, O_BX + j * 128), KC)
            slot = mm_group(bi, KC, lambda k: Cc(k), 0, n, extra_reads=bC)
            if final:
                dve(lambda e, j=j: e.tensor_copy(out=tmp(0, 0, 3), in_=BNDt[:, j, 1:4]),
                    reads=[bBND], writes=[bT[0]])
                dve(lambda e, j=j: e.tensor_copy(out=tmp(0, 1027, 48), in_=SBt[:, j, :]),
                    reads=[bSB], writes=[bT[0]])
            else:
                dve(lambda e: e.memset(tmp(0, 0, 3), 0.0), writes=[bT[0]])
            act(lambda e, slot=slot: e.activation(out=tmp(0, 3, NP_), in_=ps(slot, 0, NP_), func=AF.Copy),
                reads=[bPS[slot]], writes=[bT[0]])
            if final:
                act(lambda e, slot=slot: e.activation(out=tmp(0, 1075, NS), in_=ps(slot, NP_, NS),
                                                      func=AF.Copy),
                    reads=[bPS[slot]], writes=[bT[0]])
            dve(lambda e, j=j: e.tensor_scalar(out=tmp(1, 0, NP_), in0=tmp(0, 3, NP_),
                                               scalar1=pcol(l, 11, j), scalar2=pcol(l, 4, j),
                                               op0=ALU.mult, op1=ALU.add),
                reads=[bT[0], bPAR], writes=[bT[1]])
            for k in range(3):
                dve(lambda e, j=j, k=k: e.scalar_tensor_tensor(out=tmp(1, 0, NP_), in0=tmp(0, k, NP_),
                                                               scalar=pcol(l, 8 + k, j),
                                                               in1=tmp(1, 0, NP_),
                                                               op0=ALU.mult, op1=ALU.add),
                    reads=[bT[0], bPAR], writes=[bT[1]])
            if final:
                dve(lambda e, j=j: e.tensor_scalar(out=tmp(1, NP_, NS), in0=tmp(0, 1027 + 48, NS),
                                                   scalar1=pcol(l, 11, j), scalar2=pcol(l, 4, j),
                                                   op0=ALU.mult, op1=ALU.add),
                    reads=[bT[0], bPAR], writes=[bT[1]])
                for k in range(3):
                    dve(lambda e, j=j, k=k: e.scalar_tensor_tensor(out=tmp(1, NP_, NS),
                                                                   in0=tmp(0, 1027 + 16 * k, NS),
                                                                   scalar=pcol(l, 8 + k, j),
                                                                   in1=tmp(1, NP_, NS),
                                                                   op0=ALU.mult, op1=ALU.add),
                        reads=[bT[0], bPAR], writes=[bT[1]])
            if final:
                dve(lambda e, j=j: e.tensor_copy(out=SOt[:, j, 2:5], in_=tmp(0, NP_, 3)),
                    reads=[bT[0]], writes=[bSO])
                dve(lambda e, j=j: e.tensor_copy(out=SOt[:, j, 38:86], in_=tmp(0, 1091, 48)),
                    reads=[bT[0]], writes=[bSO])
            else:
                dve(lambda e, j=j: e.tensor_copy(out=XIt[:, j, 1:4], in_=tmp(0, NP_, 3)),
                    reads=[bT[0]], writes=[bXI])
            act(lambda e, n=n: e.activation(out=tmph(5, 0, n), in_=tmp(1, 0, n), func=AF.Copy),
                reads=[bT[1]], writes=[bT[5]])
            sr = mm_gate(0, j, n)
            act(lambda e, j=j, sr=sr, n=n: e.activation(out=tmp(2, 0, n), in_=ps(sr, 0, n), func=AF.Sigmoid,
                                                        bias=pcol(l, 5, j)),
                reads=[bPS[sr], bPAR], writes=[bT[2]])
            si = mm_gate(1, j, n)
            act(lambda e, j=j, si=si, n=n: e.activation(out=tmp(3, 0, n), in_=ps(si, 0, n), func=AF.Sigmoid,
                                                        bias=pcol(l, 6, j)),
                reads=[bPS[si], bPAR], writes=[bT[3]])
            act(lambda e, j=j, n=n: e.activation(out=tmp(4, 0, n), in_=tmp(2, 0, n), func=AF.Exp,
                                                 scale=CVt[:, l, 1, j:j + 1]),
                reads=[bT[2], bCV], writes=[bT[4]])
            act(lambda e, j=j, n=n: e.activation(out=tmp(2, 0, n), in_=tmp(2, 0, n), func=AF.Exp,
                                                 scale=CVt[:, l, 0, j:j + 1]),
                reads=[bCV], writes=[bT[2]])
            act(lambda e, n=n: e.activation(out=tmp(4, 0, n), in_=tmp(4, 0, n), func=AF.Sqrt,
                                            scale=-1.0, bias=1.0),
                writes=[bT[4]])
            dve(lambda e, n=n: e.tensor_tensor(out=tmp(3, 0, n), in0=tmp(3, 0, n), in1=tmp(1, 0, n),
                                               op=ALU.mult), reads=[bT[1]], writes=[bT[3]])
            dve(lambda e, n=n: e.tensor_tensor(out=tmp(3, 0, n), in0=tmp(3, 0, n), in1=tmp(4, 0, n),
                                               op=ALU.mult), reads=[bT[4]], writes=[bT[3]])
            if final:
                dve(lambda e, j=j: e.tensor_tensor_scan(out=tmp(4, 0, NP_), data0=tmp(2, 0, NP_),
                                                        data1=tmp(3, 0, NP_), initial=BNDt[:, j, 0:1],
                                                        op0=ALU.mult, op1=ALU.add),
                    reads=[bT[2], bT[3], bBND], writes=[bT[4]])
                for t in range(4):
                    prev = SHt[:, j, :] if t == 0 else tmp(4, NP_ + 16 * (t - 1), 16)
                    dve(lambda e, t=t, prev=prev: e.tensor_tensor(out=tmp(4, NP_ + 16 * t, 16),
                                                                  in0=tmp(2, NP_ + 16 * t, 16), in1=prev,
                                                                  op=ALU.mult),
                        reads=[bT[2], bSH], writes=[bT[4]])
                    dve(lambda e, t=t: e.tensor_tensor(out=tmp(4, NP_ + 16 * t, 16),
                                                       in0=tmp(4, NP_ + 16 * t, 16),
                                                       in1=tmp(3, NP_ + 16 * t, 16), op=ALU.add),
                        reads=[bT[3]], writes=[bT[4]])
                dve(lambda e, j=j: e.tensor_copy(out=SOt[:, j, 5:6], in_=tmp(4, NP_ - 1, 1)),
                    reads=[bT[4]], writes=[bSO])
                dve(lambda e, j=j: e.tensor_copy(out=SOt[:, j, 86:102], in_=tmp(4, T - 16, 16)),
                    reads=[bT[4]], writes=[bSO])
                bi2 = wload(wsrc(w_in, l, O_BGATE + j * 128), KC)
                s2 = mm_group(bi2, KC, lambda k: Cc(k), extra_reads=bC)
                act(lambda e, s2=s2: e.activation(out=tmp(0, 0, T), in_=ps(s2), func=AF.Square),
                    reads=[bPS[s2]], writes=[bT[0]])
                dve(lambda e: e.tensor_scalar(out=tmp(0, 0, T), in0=tmp(0, 0, T), scalar1=0.044715,
                                              scalar2=1.0, op0=ALU.mult, op1=ALU.add), writes=[bT[0]])
                dve(lambda e, s2=s2: e.tensor_tensor(out=tmp(0, 0, T), in0=ps(s2), in1=tmp(0, 0, T),
                                                     op=ALU.mult), reads=[bPS[s2]], writes=[bT[0]])
                act(lambda e: e.activation(out=tmp(0, 0, T), in_=tmp(0, 0, T), func=AF.Sigmoid,
                                           scale=1.5957691216057308), writes=[bT[0]])
                dve(lambda e, s2=s2: e.tensor_tensor(out=tmp(0, 0, T), in0=ps(s2), in1=tmp(0, 0, T),
                                                     op=ALU.mult), reads=[bPS[s2]], writes=[bT[0]])
                dve(lambda e, j=j: e.tensor_tensor(out=Dc(j), in0=tmp(0, 0, T), in1=tmp(4, 0, T),
                                                   op=ALU.mult),
                    reads=[bT[4]], writes=[bD[j]])
            else:
                dve(lambda e: e.tensor_tensor_scan(out=tmp(4, 0, NP_), data0=tmp(2, 0, NP_),
                                                   data1=tmp(3, 0, NP_), initial=0.0,
                                                   op0=ALU.mult, op1=ALU.add),
                    reads=[bT[2], bT[3]], writes=[bT[4]])
                dve(lambda e, j=j: e.tensor_copy(out=XIt[:, j, 0:1], in_=tmp(4, NP_ - 1, 1)),
                    reads=[bT[4]], writes=[bXI])

        def mm_gate(which, j, n):
            slot = pstate["n"] % 2
            pstate["n"] += 1
            tiles = [(a, min(t0 + tn, n) - a) for (t0, tn) in TILES for a in [t0] if min(t0 + tn, n) > a]
            def fn(e, slot=slot, tiles=tiles):
                ins = None
                for (a, m) in tiles:
                    ins = e.matmul(ps(slot, a, m), GWt[:, which, j, :], tmph(5, a, m), start=True, stop=True)
                return ins
            P.emit("pe", fn, reads=[bGW, bT[5]], writes=[bPS[slot]])
            return slot

        def layer(l):
            P.emit("pool", lambda e, l=l: e.dma_start(out=GWt[:, 0, :, :],
                                                      in_=w_r[l].rearrange("b i j -> i b j")),
                   writes=[bGW], sig="D_gw", inc=16)
            P.emit("pool", lambda e, l=l: e.dma_start(out=GWt[:, 1, :, :],
                                                      in_=w_i[l].rearrange("b i j -> i b j")),
                   writes=[bGW], sig="D_gw", inc=16)
            sp_dma(lambda e, l=l: e.dma_start(out=SAt[:], in_=stA[l].rearrange("(c p) n -> p c n", p=128)),
                   "D_st", writes=[bSA])
            sp_dma(lambda e, l=l: e.dma_start(out=SBt[:], in_=stB[l].rearrange("(c p) n -> p c n", p=128)),
                   "D_st", writes=[bSB])
            tk3 = sp_dma(lambda e, l=l: e.dma_start(out=SHt[:], in_=stH[l].rearrange("(c p) n -> p c n", p=128)),
                         "D_st", writes=[bSH])
            bSA.w = tk3
            bSB.w = tk3

            norm_stats(lambda j: Bf[:, j, :], bB, Dc, bD)
            for j in range(KC):
                dve(lambda e, j=j: e.scalar_tensor_tensor(out=Cc(j), in0=Bf[:, j, :], scalar=pcol(l, 0, j),
                                                          in1=RSt[:], op0=ALU.mult, op1=ALU.mult),
                    reads=[bB[j], bRS, bPAR], writes=[bC[j]])
            spill_x()
            if STOP <= 1:
                return

            for j in range(KC):
                lru_chunk(l, j, final=False)

            if STOP <= 2:
                return
            for j in range(8):
                b_cg = wload(wsrc(w_in, l, O_CG + j * 128), KC)
                s_cg = mm_group(b_cg, KC, lambda k: Cc(k), extra_reads=bC)
                act(lambda e, s=s_cg: e.activation(out=tmp(1), in_=ps(s), func=AF.Copy),
                    reads=[bPS[s_cg]], writes=[bT[1]])
                b_ax = wload(wsrc(w_in, l, O_AX + j * 128), KC)
                s_ax = mm_group(b_ax, KC, lambda k: Cc(k), extra_reads=bC)
                dve(lambda e, j=j, s=s_ax: e.tensor_tensor(out=Bf[:, j, :], in0=ps(s), in1=tmp(1), op=ALU.mult),
                    reads=[bPS[s_ax], bT[1]], writes=[bB[j]])
                dve(lambda e, j=j: e.tensor_copy(out=XIt[:, j, 4:6], in_=Bf[:, j, NP_ - 2:NP_]),
                    reads=[bB[j]], writes=[bXI])
            if STOP <= 3:
                return
            P.emit("pool", lambda e: e.dma_start(out=cc_in.ap(), in_=XIt[:].rearrange("p c n -> p (c n)")),
                   reads=[bXI], writes=[bCC], sig="D_xch", inc=16)
            P.emit("pool", lambda e: e.collective_compute("AllGather", ALU.bypass, replica_groups=pairs,
                                                          ins=[cc_in.ap()], outs=[cc_out.ap()]),
                   writes=[bCC], sig="CC", inc=1)
            P.emit("pool", lambda e: e.dma_start(out=XRt[:].rearrange("p c n -> p (c n)"),
                                                 in_=cc_out.ap()[0:128, :]),
                   reads=[bCC], writes=[bXR], sig="D_xch", inc=16)
            dve(lambda e: e.tensor_scalar(out=BNDt[:].rearrange("p c n -> p (c n)"),
                                          in0=XRt[:].rearrange("p c n -> p (c n)"),
                                          scalar1=MASKt[:, 0:1], scalar2=None, op0=ALU.mult),
                reads=[bXR, bMASK], writes=[bBND])
            for j in range(8):
                dve(lambda e, j=j: e.tensor_copy(out=tmp(0, 0, 2), in_=BNDt[:, j, 4:6]),
                    reads=[bBND], writes=[bT[0]])
                dve(lambda e, j=j: e.tensor_copy(out=tmp(0, 1026, 32), in_=SAt[:, j, :]),
                    reads=[bSA], writes=[bT[0]])
                act(lambda e, j=j: e.activation(out=tmp(0, 2, NP_), in_=Bf[:, j, 0:NP_], func=AF.Copy),
                    reads=[bB[j]], writes=[bT[0]])
                act(lambda e, j=j: e.activation(out=tmp(0, 1058, NS), in_=Bf[:, j, NP_:T], func=AF.Copy),
                    reads=[bB[j]], writes=[bT[0]])
                dve(lambda e, j=j: e.tensor_scalar(out=tmp(2, 0, NP_), in0=tmp(0, 2, NP_),
                                                   scalar1=pcol(l, 14, j), scalar2=None, op0=ALU.mult),
                    reads=[bT[0], bPAR], writes=[bT[2]])
                dve(lambda e, j=j: e.tensor_scalar(out=tmp(2, NP_, NS), in0=tmp(0, 1058, NS),
                                                   scalar1=pcol(l, 14, j), scalar2=None, op0=ALU.mult),
                    reads=[bT[0], bPAR], writes=[bT[2]])
                for k in range(2):
                    dve(lambda e, j=j, k=k: e.scalar_tensor_tensor(out=tmp(2, 0, NP_), in0=tmp(0, k, NP_),
                                                                   scalar=pcol(l, 12 + k, j),
                                                                   in1=tmp(2, 0, NP_),
                                                                   op0=ALU.mult, op1=ALU.add),
                        reads=[bT[0], bPAR], writes=[bT[2]])
                    dve(lambda e, j=j, k=k: e.scalar_tensor_tensor(out=tmp(2, NP_, NS),
                                                                   in0=tmp(0, 1026 + 16 * k, NS),
                                                                   scalar=pcol(l, 12 + k, j),
                                                                   in1=tmp(2, NP_, NS),
                                                                   op0=ALU.mult, op1=ALU.add),
                        reads=[bT[0], bPAR], writes=[bT[2]])
                dve(lambda e, j=j: e.tensor_copy(out=SOt[:, j, 0:2], in_=tmp(0, NP_, 2)),
                    reads=[bT[0]], writes=[bSO])
                dve(lambda e, j=j: e.tensor_copy(out=SOt[:, j, 6:38], in_=tmp(0, 1090, 32)),
                    reads=[bT[0]], writes=[bSO])
                b_bg = wload(wsrc(w_in, l, O_BG + j * 128), KC)
                s_bg = mm_group(b_bg, KC, lambda k: Cc(k), extra_reads=bC)
                dve(lambda e, j=j, s=s_bg: e.tensor_tensor(out=cach(j), in0=ps(s), in1=tmp(2), op=ALU.mult),
                    reads=[bPS[s_bg], bT[2]], writes=[bBh(16 + j)])

            if STOP <= 4:
                return
            for j in range(KC):
                lru_chunk(l, j, final=True)

            if STOP <= 5:
                return
            for j in range(KC):
                b1 = wload(wsrc(w_in, l, O_GA + j * 128), KC)
                s1 = mm_group(b1, KC, lambda k: Cc(k), extra_reads=bC)
                act(lambda e, s=s1: e.activation(out=tmp(0), in_=ps(s), func=AF.Sigmoid),
                    reads=[bPS[s1]], writes=[bT[0]])
                b2 = wload(wsrc(w_out_a, l, j * 128, kc=8), 8)
                s2 = mm_group(b2, 8, lambda k: cach(k), extra_reads=[bBh(16 + k) for k in range(8)])
                dve(lambda e, s=s2: e.tensor_tensor(out=tmp(1), in0=ps(s), in1=tmp(0), op=ALU.mult),
                    reads=[bPS[s2], bT[0]], writes=[bT[1]])
                b3 = wload(wsrc(w_in, l, O_GB + j * 128), KC)
                s3 = mm_group(b3, KC, lambda k: Cc(k), extra_reads=bC)
                act(lambda e, s=s3: e.activation(out=tmp(2), in_=ps(s), func=AF.Sigmoid),
                    reads=[bPS[s3]], writes=[bT[2]])
                b4 = wload(wsrc(w_out_b, l, j * 128), KC)
                s4 = mm_group(b4, KC, lambda k: Dc(k), extra_reads=bD)
                dve(lambda e, s=s4: e.tensor_tensor(out=tmp(3), in0=ps(s), in1=tmp(2), op=ALU.mult),
                    reads=[bPS[s4], bT[2]], writes=[bT[3]])
                dve(lambda e, j=j: e.tensor_tensor(out=merged(j), in0=tmp(3), in1=tmp(1), op=ALU.add),
                    reads=[bT[3], bT[1]], writes=[bBh(j)])

            if STOP <= 6:
                return
            allmerged = [bBh(k) for k in range(KC)]
            for jo in range(KC):
                b1 = wload(wsrc(w_o, l, jo * 128), KC)
                s1 = mm_group(b1, KC, lambda k: merged(k), extra_reads=allmerged)
                wr = [bC[jo // 2]] if jo < 16 else []
                owners = [bC[2 * jo], bC[2 * jo + 1]] if jo < 8 else [bD[2 * jo - 16], bD[2 * jo - 15]]
                act(lambda e, jo=jo, s=s1: e.activation(out=CDf[:, jo, :], in_=ps(s), func=AF.Copy),
                    reads=[bPS[s1]], writes=owners)
                act(lambda e, jo=jo, s=s1: e.activation(out=sqB(jo), in_=ps(s), func=AF.Square),
                    reads=[bPS[s1]], writes=[bBh(16 + jo)])
            slot = pstate["n"] % 2
            pstate["n"] += 1
            def fn_ss(e, slot=slot):
                ins = None
                for k in range(KC):
                    for (a, m) in TILES:
                        ins = e.matmul(ps(slot, a, m), ONESt[:], sqB(k)[:, a:a + m],
                                       start=(k == 0), stop=(k == KC - 1))
                return ins
            P.emit("pe", fn_ss, reads=[bBh(16 + k) for k in range(KC)] + [bONES], writes=[bPS[slot]])
            act(lambda e, slot=slot: e.activation(out=RSt[:], in_=ps(slot), func=AF.Sqrt, scale=1.0 / DM,
                                                  bias=EPS), reads=[bPS[slot]], writes=[bRS])
            dve(lambda e: e.reciprocal(out=RSt[:], in_=RSt[:]), writes=[bRS])
            mixbufs = [None] * KC
            class _MB:
                pass
            def mixowner(jo):
                return bC[2 * jo] if jo < 8 else bD[2 * jo - 16]
            def mixowner2(jo):
                return bC[2 * jo + 1] if jo < 8 else bD[2 * jo - 15]
            xsv = xs.rearrange("(c p) t -> p c t", p=128)
            for j in range(KC):
                st = 4 + (j % 2)
                sp_dma(lambda e, j=j, st=st: e.dma_start(out=tmp(st), in_=xsv[:, j, :]),
                       "D_xs%d" % (j % 2), reads=[bXS], writes=[bT[st]])
                dve(lambda e, j=j: e.scalar_tensor_tensor(out=tmp(3), in0=CDf[:, j, :], scalar=pcol(l, 1, j),
                                                          in1=RSt[:], op0=ALU.mult, op1=ALU.mult),
                    reads=[mixowner(j), mixowner2(j), bRS, bPAR], writes=[bT[3]])
                dve(lambda e, j=j, st=st: e.tensor_tensor(out=Bf[:, j, :], in0=tmp(3), in1=tmp(st), op=ALU.add),
                    reads=[bT[3], bT[st]], writes=[bB[j]])

            if STOP <= 7:
                return
            norm_stats(lambda j: Bf[:, j, :], bB, Dc, bD)
            for j in range(KC):
                dve(lambda e, j=j: e.scalar_tensor_tensor(out=Cc(j), in0=Bf[:, j, :], scalar=pcol(l, 2, j),
                                                          in1=RSt[:], op0=ALU.mult, op1=ALU.mult),
                    reads=[bB[j], bRS, bPAR], writes=[bC[j]])
            spill_x()
            for bidx, (f0, nf) in enumerate(FBLK):
                for ff in range(nf):
                    f = f0 + ff
                    b1 = wload(wsrc(w_gu, l, f * 128), KC)
                    s1 = mm_group(b1, KC, lambda k: Cc(k), extra_reads=bC)
                    act(lambda e, s=s1, ff=ff: e.activation(out=tmp(ff % 2), in_=ps(s), func=AF.Silu),
                        reads=[bPS[s1]], writes=[bT[ff % 2]])
                    b2 = wload(wsrc(w_gu, l, FF + f * 128), KC)
                    s2 = mm_group(b2, KC, lambda k: Cc(k), extra_reads=bC)
                    dve(lambda e, ff=ff, s=s2: e.tensor_tensor(out=Dc(ff), in0=ps(s), in1=tmp(ff % 2),
                                                               op=ALU.mult),
                        reads=[bPS[s2], bT[ff % 2]], writes=[bD[ff]])
                for jo in range(KC):
                    b1 = wload(wsrc(w_down, l, jo * 128, kc=nf, row0=f0 * 128), nf)
                    s1 = mm_group(b1, nf, lambda k: Dc(k), extra_reads=bD[:nf])
                    if bidx == 0:
                        act(lambda e, jo=jo, s=s1: e.activation(out=Bf[:, jo, :], in_=ps(s), func=AF.Copy),
                            reads=[bPS[s1]], writes=[bB[jo]])
                    else:
                        dve(lambda e, jo=jo, s=s1: e.tensor_tensor(out=Bf[:, jo, :], in0=ps(s),
                                                                   in1=Bf[:, jo, :], op=ALU.add),
                            reads=[bPS[s1]], writes=[bB[jo]])
            norm_stats(lambda j: Bf[:, j, :], bB, Cc, bC)
            residual_update(l, lambda j: Bf[:, j, :], bB, 3)

            sp_dma(lambda e, l=l: e.dma_start(out=sto[l].rearrange("(c p) n -> p c n", p=128), in_=SOt[:]),
                   "D_out", reads=[bSO], writes=[])

        for l_ in range(DEPTH):
            layer(l_)

        yTv = yT.rearrange("(c p) t -> p c t", p=128)
        last = None
        for j in range(KC):
            last = sp_dma(lambda e, j=j: e.dma_start(out=yTv[:, j, :], in_=Bf[:, j, :]), "D_out",
                          reads=[bB[j]], writes=[])
        final_waits = [("D_out", P.cnt["D_out"])]

        with nc.Block() as block:
            def replay(eng_key, e, tail=()):
                for (waits, fn, sig, inc) in P.q[eng_key]:
                    for (k, v) in waits:
                        e.wait_ge(sem[k], v)
                    ins = fn(e)
                    ins.then_inc(sem[sig], inc)
                for (k, v) in tail:
                    e.wait_ge(sem[k], v)

            @block.sync
            def _(e):
                replay("sp", e, tail=final_waits)

            @block.gpsimd
            def _(e):
                replay("pool", e)

            @block.scalar
            def _(e):
                replay("act", e)

            @block.vector
            def _(e):
                replay("dve", e)

            @block.tensor
            def _(e):
                replay("pe", e)
    return nc


_NC_CACHE = {}


def kernel(x_prompt, x_sample, state_conv_a, state_conv_b, state_lru_h, w_in, conv_a_w, w_out_a,
           conv_b_w, conv_b_bias, w_r, b_r, w_i, b_i, lru_lambda, w_out_b, w_o, norm_pre_mix,
           norm_post_mix, norm_pre_ffn, norm_post_ffn, w_gate_up, w_down):
    f32 = np.float32
    A = lambda a: np.ascontiguousarray(np.asarray(a), dtype=f32)
    x_prompt, x_sample = A(x_prompt), A(x_sample)
    state_conv_a, state_conv_b, state_lru_h = A(state_conv_a), A(state_conv_b), A(state_lru_h)
    DEPTH = DEBUG.get("depth", 4)
    if "nc" not in _NC_CACHE:
        _NC_CACHE["nc"] = build_program()
    nc = _NC_CACHE["nc"]

    def chan(v):
        v = A(v)
        c = v.shape[1] // 128
        o = np.zeros((128, DEPTH, 16), f32)
        o[:, :, :c] = v.reshape(DEPTH, c, 128).transpose(2, 0, 1)
        return o
    cbw, caw = A(conv_b_w), A(conv_a_w)
    plist = [norm_pre_mix, norm_post_mix, norm_pre_ffn, norm_post_ffn, conv_b_bias, b_r, b_i, lru_lambda,
             cbw[:, 0], cbw[:, 1], cbw[:, 2], cbw[:, 3], caw[:, 0], caw[:, 1], caw[:, 2]]
    par = np.stack([chan(p) for p in plist], axis=2)
    par = np.ascontiguousarray(par.reshape(128, DEPTH * NPAR * 16))

    if DEBUG.get("lite", False):
        w_out_a, w_out_b, w_o = w_out_a[:, :128, :128], w_out_b[:, :128, :128], w_o[:, :128, :128]
        w_gate_up, w_down = w_gate_up[:, :128, :128], w_down[:, :128, :128]
    shared = {"w_in": A(w_in), "w_out_a": A(w_out_a), "w_r": A(w_r), "w_i": A(w_i), "w_out_b": A(w_out_b),
              "w_o": A(w_o), "w_gate_up": A(w_gate_up), "w_down": A(w_down), "par": par}
    in_maps = []
    for c in range(8):
        s, half = c // 2, c % 2
        xp = x_prompt[s, half * 1024:(half + 1) * 1024, :]
        xsm = x_sample[16 * c:16 * c + 16].transpose(1, 0, 2).reshape(64, DM)
        xT = np.ascontiguousarray(np.concatenate([xp, xsm], axis=0).T)
        sa = state_conv_a[:, 16 * c:16 * c + 16]
        stA = np.ascontiguousarray(sa.transpose(0, 3, 2, 1).reshape(DEPTH, DA, 32))
        sbb = state_conv_b[:, 16 * c:16 * c + 16]
        stB = np.ascontiguousarray(sbb.transpose(0, 3, 2, 1).reshape(DEPTH, DM, 48))
        sh = state_lru_h[:, 16 * c:16 * c + 16]
        stH = np.ascontiguousarray(sh.transpose(0, 2, 1))
        m = dict(shared)
        m.update({"xT": xT, "stA": stA, "stB": stB, "stH": stH,
                  "mask": np.full((128, 1), float(half), f32)})
        in_maps.append(m)

    if DEBUG.get("trace"):
        res = run_bass_kernel_spmd(nc, in_maps, core_ids=list(range(8)), trace=True)
        print("EXEC_TIME_NS", res.exec_time_ns)
    else:
        res = run_bass_kernel_spmd(nc, in_maps, core_ids=list(range(8)))
    outs = res.results
    DEBUG["last_outs"] = outs

    y_prompt = np.zeros((4, 2048, DM), f32)
    y_sample = np.zeros((128, 4, DM), f32)
    pa = np.zeros((DEPTH, 4, 2, DA), f32)
    pb = np.zeros((DEPTH, 4, 3, DM), f32)
    ph = np.zeros((DEPTH, 4, DM), f32)
    sa_o = np.zeros((DEPTH, 128, 2, DA), f32)
    sb_o = np.zeros((DEPTH, 128, 3, DM), f32)
    sh_o = np.zeros((DEPTH, 128, DM), f32)
    for c in range(8):
        s, half = c // 2, c % 2
        yT = np.asarray(outs[c]["yT"])
        y_prompt[s, half * 1024:(half + 1) * 1024, :] = yT[:, :1024].T
        y_sample[16 * c:16 * c + 16] = yT[:, 1024:].T.reshape(4, 16, DM).transpose(1, 0, 2)
        so = np.asarray(outs[c]["sto"])
        if half == 1:
            pa[:, s] = so[:, :DA, 0:2].transpose(0, 2, 1)
            pb[:, s] = so[:, :, 2:5].transpose(0, 2, 1)
            ph[:, s] = so[:, :, 5]
        sa_o[:, 16 * c:16 * c + 16] = so[:, :DA, 6:38].reshape(DEPTH, DA, 2, 16).transpose(0, 3, 2, 1)
        sb_o[:, 16 * c:16 * c + 16] = so[:, :, 38:86].reshape(DEPTH, DM, 3, 16).transpose(0, 3, 2, 1)
        sh_o[:, 16 * c:16 * c + 16] = so[:, :, 86:102].transpose(0, 2, 1)
    return (y_prompt, y_sample, pa, pb, ph, sa_o, sb_o, sh_o)
```
